# Optimizing a Trainium2 kernel written in Bass

```python
import jax, jax.numpy as jnp
from jax import lax
import numpy as np

D_MODEL = 2048
BATCH = 2
SEQ = 8192
DEPTH = 4

SSD_HEADS = 32
SSD_HEAD_DIM = 64
SSD_WIDTH = SSD_HEADS * SSD_HEAD_DIM
SSD_GROUPS = 8
SSD_STATE = 128
SSD_CONV = 4
SSD_CHUNK = 128
SSD_XBC = SSD_WIDTH + 2 * SSD_GROUPS * SSD_STATE
SSD_NORM_EPS = 1e-5

S5_GROUP_CH = 16
S5_GROUPS = 64
S5_WIDTH = S5_GROUPS * S5_GROUP_CH
S5_STATE = 64

RWKV_HEADS = 16
RWKV_HEAD_DIM = 64
RWKV_WIDTH = RWKV_HEADS * RWKV_HEAD_DIM
RWKV_LORA = 64
RWKV_LNX_EPS = 64e-5

N_BRANCHES = 3
COL_SIZES = (SSD_WIDTH, SSD_XBC, SSD_HEADS, S5_WIDTH, S5_WIDTH, 4 * RWKV_WIDTH, RWKV_LORA, RWKV_LORA, N_BRANCHES * D_MODEL)
IN_COLS = sum(COL_SIZES)
DEEPNORM_ALPHA = (2 * DEPTH) ** 0.25
DEEPNORM_BETA = (8 * DEPTH) ** -0.25
LN_EPS = 1e-5

kernel_name = 'hybrid_ssd_s5_rwkv7_gated_deepnorm'


def _column_offsets():
    offs, acc = [], 0
    for c in COL_SIZES[:-1]:
        acc += c
        offs.append(acc)
    return offs


def layer_norm(h, g, b):
    hf = h.astype(jnp.float32)
    mu = jnp.mean(hf, -1, keepdims=True)
    var = jnp.mean(jnp.square(hf - mu), -1, keepdims=True)
    return ((hf - mu) * lax.rsqrt(var + LN_EPS) * g + b).astype(h.dtype)


def token_shift(p):
    return jnp.concatenate([jnp.zeros_like(p[:, :1]), p[:, :-1]], axis=1)


def causal_depthwise_conv(u, w, bias):
    k, c = w.shape
    out = lax.conv_general_dilated(u, w.astype(u.dtype)[:, None, :], window_strides=(1,),
                                   padding=[(k - 1, 0)], dimension_numbers=('NWC', 'WIO', 'NWC'),
                                   feature_group_count=c)
    return out + bias


def segsum(a):
    t = a.shape[-1]
    cs = jnp.cumsum(a, -1)
    seg = cs[..., :, None] - cs[..., None, :]
    mask = jnp.tril(jnp.ones((t, t), bool))
    return jnp.where(mask, seg, -jnp.inf)


def ssd_chunked(xh, dt, a, bm, cm):
    b, s, h, p = xh.shape
    g, n = bm.shape[-2:]
    j = h // g
    c, l = s // SSD_CHUNK, SSD_CHUNK
    x = (xh * dt[..., None]).reshape(b, c, l, g, j, p)
    da = (dt * a).reshape(b, c, l, g, j).transpose(0, 3, 4, 1, 2)
    bc = bm.reshape(b, c, l, g, n)
    cc = cm.reshape(b, c, l, g, n)
    cs = jnp.cumsum(da, -1)
    lmat = jnp.exp(segsum(da))
    cb = jnp.einsum('bclgn,bcsgn->bcgls', cc, bc)
    y_diag = jnp.einsum('bcgls,bgjcls,bcsgjp->bclgjp', cb, lmat, x)
    decay_states = jnp.exp(cs[..., -1:] - cs)
    states = jnp.einsum('bclgn,bgjcl,bclgjp->bcgjpn', bc, decay_states, x)
    chunk_tot = jnp.pad(cs[..., -1], ((0, 0), (0, 0), (0, 0), (1, 0)))
    decay_chunk = jnp.exp(segsum(chunk_tot))
    states = jnp.concatenate([jnp.zeros_like(states[:, :1]), states], axis=1)
    states = jnp.einsum('bgjzc,bcgjpn->bzgjpn', decay_chunk, states)[:, :-1]
    y_off = jnp.einsum('bclgn,bcgjpn,bgjcl->bclgjp', cc, states, jnp.exp(cs))
    return (y_diag + y_off).reshape(b, s, h, p)


def mamba2_branch(z, xbc, dt_raw, conv_w, conv_b, dt_bias, a_log, d_skip, norm_w):
    b, s, _ = z.shape
    xbc = jax.nn.silu(causal_depthwise_conv(xbc, conv_w, conv_b))
    xs, bm, cm = jnp.split(xbc, [SSD_WIDTH, SSD_WIDTH + SSD_GROUPS * SSD_STATE], axis=-1)
    xh = xs.reshape(b, s, SSD_HEADS, SSD_HEAD_DIM)
    bm = bm.reshape(b, s, SSD_GROUPS, SSD_STATE)
    cm = cm.reshape(b, s, SSD_GROUPS, SSD_STATE)
    dt = jax.nn.softplus(dt_raw + dt_bias)
    a = -jnp.exp(a_log.astype(jnp.float32))
    y = ssd_chunked(xh, dt, a, bm, cm) + d_skip[:, None] * xh
    y = y.reshape(b, s, SSD_WIDTH) * jax.nn.silu(z)
    yg = y.reshape(b, s, SSD_GROUPS, SSD_WIDTH // SSD_GROUPS)
    yg = yg * lax.rsqrt(jnp.mean(jnp.square(yg), -1, keepdims=True) + SSD_NORM_EPS)
    return yg.reshape(b, s, SSD_WIDTH) * norm_w


def s5_branch(u, zb, lam_re, lam_im, log_dt, b_re, b_im, c_re, c_im, d_s5, w_glu, b_glu):
    b, s, _ = u.shape
    ug = u.reshape(b, s, S5_GROUPS, S5_GROUP_CH)
    dt = jnp.exp(log_dt)[:, None]
    mag = jnp.exp(lam_re * dt)
    ang = lam_im * dt
    lb_re, lb_im = mag * jnp.cos(ang), mag * jnp.sin(ang)
    den = jnp.square(lam_re) + jnp.square(lam_im)
    nr, ni = lb_re - 1.0, lb_im
    q_re = (nr * lam_re + ni * lam_im) / den
    q_im = (ni * lam_re - nr * lam_im) / den
    bb_re = q_re[..., None] * b_re - q_im[..., None] * b_im
    bb_im = q_re[..., None] * b_im + q_im[..., None] * b_re
    bu_re = jnp.einsum('bsgh,gph->bsgp', ug, bb_re)
    bu_im = jnp.einsum('bsgh,gph->bsgp', ug, bb_im)
    a_re = jnp.broadcast_to(lb_re, bu_re.shape)
    a_im = jnp.broadcast_to(lb_im, bu_im.shape)

    def combine(e1, e2):
        a1r, a1i, b1r, b1i = e1
        a2r, a2i, b2r, b2i = e2
        return (a2r * a1r - a2i * a1i, a2r * a1i + a2i * a1r,
                a2r * b1r - a2i * b1i + b2r, a2r * b1i + a2i * b1r + b2i)

    _, _, st_re, st_im = lax.associative_scan(combine, (a_re, a_im, bu_re, bu_im), axis=1)
    y = jnp.einsum('bsgp,ghp->bsgh', st_re, c_re) - jnp.einsum('bsgp,ghp->bsgh', st_im, c_im)
    y = y.reshape(b, s, S5_WIDTH) + d_s5 * u
    y = jax.nn.gelu(y)
    y = y * jax.nn.sigmoid(y @ w_glu + b_glu)
    return y * jax.nn.silu(zb)


def rwkv7_recurrence(r, w, k, v, a, bvec):
    b, s, h, n = r.shape

    def step(state, inp):
        r_t, w_t, k_t, v_t, a_t, b_t = inp
        sa = jnp.einsum('bhvk,bhk->bhv', state, a_t)
        state = (state * w_t[:, :, None, :] + sa[..., None] * b_t[:, :, None, :]
                 + v_t[..., None] * k_t[:, :, None, :])
        return state, jnp.einsum('bhvk,bhk->bhv', state, r_t)

    xs = tuple(jnp.moveaxis(t, 1, 0) for t in (r, w, k, v, a, bvec))
    s0 = jnp.zeros((b, h, n, n), jnp.float32)
    _, ys = lax.scan(step, s0, xs)
    return jnp.moveaxis(ys, 0, 1)


def rwkv7_branch(p_rkvz, p_w, p_a, mu, mu_lora, w0, w2, a0, a2, k_k, k_a, r_k, lnx_w, lnx_b):
    b, s, _ = p_rkvz.shape
    heads = lambda t: t.reshape(b, s, RWKV_HEADS, RWKV_HEAD_DIM)
    p4 = p_rkvz.reshape(b, s, 4, RWKV_WIDTH)
    p4 = p4 + (token_shift(p4) - p4) * mu
    r, k, v, zc = p4[:, :, 0], p4[:, :, 1], p4[:, :, 2], p4[:, :, 3]
    xw = p_w + (token_shift(p_w) - p_w) * mu_lora[0]
    xa = p_a + (token_shift(p_a) - p_a) * mu_lora[1]
    w_log = -jax.nn.softplus(-(w0 + jnp.tanh(xw) @ w2)) - 0.5
    decay = jnp.exp(-jnp.exp(w_log))
    a = jax.nn.sigmoid(a0 + xa @ a2)
    kk = heads(k * k_k)
    kk = kk * lax.rsqrt(jnp.maximum(jnp.sum(jnp.square(kk), -1, keepdims=True), 1e-24))
    k = k * (1.0 + (a - 1.0) * k_a)
    rh, kh, vh = heads(r), heads(k), heads(v)
    y = rwkv7_recurrence(rh, heads(decay), kh, vh, -kk, kk * heads(a))
    mean = jnp.mean(y, -1, keepdims=True)
    var = jnp.mean(jnp.square(y - mean), -1, keepdims=True)
    y = ((y - mean) * lax.rsqrt(var + RWKV_LNX_EPS)).reshape(b, s, RWKV_WIDTH) * lnx_w + lnx_b
    y = y + (jnp.sum(rh * kh * r_k, -1, keepdims=True) * vh).reshape(b, s, RWKV_WIDTH)
    return y * jax.nn.silu(zc)


def setup_inputs(seed: int = 0) -> dict:
    key = jax.random.key(seed)
    ks = iter(jax.random.split(key, 48))
    nrm = lambda shape, sc: sc * jax.random.normal(next(ks), shape, jnp.float32)
    uni = lambda shape, lo, hi: jax.random.uniform(next(ks), shape, jnp.float32, lo, hi)
    L = DEPTH
    x = nrm((BATCH, SEQ, D_MODEL), 1.0)
    w_in = nrm((L, D_MODEL, IN_COLS), D_MODEL ** -0.5)
    ssd_conv_w = nrm((L, SSD_CONV, SSD_XBC), SSD_CONV ** -0.5)
    ssd_conv_b = nrm((L, SSD_XBC), 0.02)
    dt0 = jnp.exp(uni((L, SSD_HEADS), float(np.log(1e-3)), float(np.log(1e-1))))
    ssd_dt_bias = dt0 + jnp.log(-jnp.expm1(-dt0))
    ssd_a_log = jnp.log(uni((L, SSD_HEADS), 1.0, 16.0))
    ssd_d = 1.0 + nrm((L, SSD_HEADS), 0.1)
    ssd_norm_w = 1.0 + nrm((L, SSD_WIDTH), 0.05)
    n_idx = jnp.arange(S5_STATE, dtype=jnp.float32)
    s5_lambda_re = -0.5 + nrm((L, S5_GROUPS, S5_STATE), 0.01)
    s5_lambda_im = jnp.pi * n_idx + nrm((L, S5_GROUPS, S5_STATE), 0.01)
    s5_log_dt = uni((L, S5_GROUPS), float(np.log(1e-3)), float(np.log(1e-1)))
    s5_b_re = nrm((L, S5_GROUPS, S5_STATE, S5_GROUP_CH), (2 * S5_GROUP_CH) ** -0.5)
    s5_b_im = nrm((L, S5_GROUPS, S5_STATE, S5_GROUP_CH), (2 * S5_GROUP_CH) ** -0.5)
    s5_c_re = nrm((L, S5_GROUPS, S5_GROUP_CH, S5_STATE), (2 * S5_STATE) ** -0.5)
    s5_c_im = nrm((L, S5_GROUPS, S5_GROUP_CH, S5_STATE), (2 * S5_STATE) ** -0.5)
    s5_d = nrm((L, S5_WIDTH), 1.0)
    s5_w_glu = nrm((L, S5_WIDTH, S5_WIDTH), S5_WIDTH ** -0.5)
    s5_b_glu = nrm((L, S5_WIDTH), 0.02)
    rwkv_mu = uni((L, 4, RWKV_WIDTH), 0.0, 1.0)
    rwkv_mu_lora = uni((L, 2, RWKV_LORA), 0.0, 1.0)
    ratio = jnp.arange(L, dtype=jnp.float32) / max(L - 1, 1)
    ch = jnp.arange(RWKV_WIDTH, dtype=jnp.float32) / (RWKV_WIDTH - 1)
    rwkv_w0 = -6.5 + 5.0 * ch[None, :] ** (0.85 + jnp.sqrt(ratio)[:, None]) + nrm((L, RWKV_WIDTH), 0.1)
    rwkv_w2 = nrm((L, RWKV_LORA, RWKV_WIDTH), 0.1 * RWKV_LORA ** -0.5)
    rwkv_a0 = nrm((L, RWKV_WIDTH), 0.1)
    rwkv_a2 = nrm((L, RWKV_LORA, RWKV_WIDTH), 0.1 * RWKV_LORA ** -0.5)
    rwkv_k_k = 0.85 + nrm((L, RWKV_WIDTH), 0.05)
    rwkv_k_a = 1.0 + nrm((L, RWKV_WIDTH), 0.05)
    rwkv_r_k = nrm((L, RWKV_HEADS, RWKV_HEAD_DIM), 0.1)
    rwkv_lnx_w = 1.0 + nrm((L, RWKV_WIDTH), 0.05)
    rwkv_lnx_b = nrm((L, RWKV_WIDTH), 0.02)
    gate_b = nrm((L, N_BRANCHES, D_MODEL), 0.02)
    w_branch_a = nrm((L, SSD_WIDTH, D_MODEL), DEEPNORM_BETA * SSD_WIDTH ** -0.5)
    w_branch_b = nrm((L, S5_WIDTH, D_MODEL), DEEPNORM_BETA * S5_WIDTH ** -0.5)
    w_branch_c = nrm((L, RWKV_WIDTH, D_MODEL), DEEPNORM_BETA * RWKV_WIDTH ** -0.5)
    w_out = nrm((L, D_MODEL, D_MODEL), DEEPNORM_BETA * D_MODEL ** -0.5)
    ln_g = 1.0 + nrm((L, D_MODEL), 0.05)
    ln_b = nrm((L, D_MODEL), 0.02)
    return {'x': x, 'w_in': w_in, 'ssd_conv_w': ssd_conv_w, 'ssd_conv_b': ssd_conv_b,
            'ssd_dt_bias': ssd_dt_bias, 'ssd_a_log': ssd_a_log, 'ssd_d': ssd_d, 'ssd_norm_w': ssd_norm_w,
            's5_lambda_re': s5_lambda_re, 's5_lambda_im': s5_lambda_im, 's5_log_dt': s5_log_dt,
            's5_b_re': s5_b_re, 's5_b_im': s5_b_im, 's5_c_re': s5_c_re, 's5_c_im': s5_c_im,
            's5_d': s5_d, 's5_w_glu': s5_w_glu, 's5_b_glu': s5_b_glu,
            'rwkv_mu': rwkv_mu, 'rwkv_mu_lora': rwkv_mu_lora, 'rwkv_w0': rwkv_w0, 'rwkv_w2': rwkv_w2,
            'rwkv_a0': rwkv_a0, 'rwkv_a2': rwkv_a2, 'rwkv_k_k': rwkv_k_k, 'rwkv_k_a': rwkv_k_a,
            'rwkv_r_k': rwkv_r_k, 'rwkv_lnx_w': rwkv_lnx_w, 'rwkv_lnx_b': rwkv_lnx_b,
            'gate_b': gate_b, 'w_branch_a': w_branch_a, 'w_branch_b': w_branch_b, 'w_branch_c': w_branch_c,
            'w_out': w_out, 'ln_g': ln_g, 'ln_b': ln_b}


def reference(x, w_in, ssd_conv_w, ssd_conv_b, ssd_dt_bias, ssd_a_log, ssd_d, ssd_norm_w,
              s5_lambda_re, s5_lambda_im, s5_log_dt, s5_b_re, s5_b_im, s5_c_re, s5_c_im,
              s5_d, s5_w_glu, s5_b_glu,
              rwkv_mu, rwkv_mu_lora, rwkv_w0, rwkv_w2, rwkv_a0, rwkv_a2, rwkv_k_k, rwkv_k_a,
              rwkv_r_k, rwkv_lnx_w, rwkv_lnx_b,
              gate_b, w_branch_a, w_branch_b, w_branch_c, w_out, ln_g, ln_b):
    b, s, d = x.shape
    offs = _column_offsets()
    for l in range(DEPTH):
        proj = jnp.einsum('bsd,dc->bsc', x, w_in[l]).astype(jnp.float32)
        z_a, xbc, dt_raw, u_b, z_b, p_rkvz, p_w, p_a, p_gate = jnp.split(proj, offs, axis=-1)
        y_a = mamba2_branch(z_a, xbc, dt_raw, ssd_conv_w[l], ssd_conv_b[l], ssd_dt_bias[l],
                            ssd_a_log[l], ssd_d[l], ssd_norm_w[l])
        y_b = s5_branch(u_b, z_b, s5_lambda_re[l], s5_lambda_im[l], s5_log_dt[l], s5_b_re[l],
                        s5_b_im[l], s5_c_re[l], s5_c_im[l], s5_d[l], s5_w_glu[l], s5_b_glu[l])
        y_c = rwkv7_branch(p_rkvz, p_w, p_a, rwkv_mu[l], rwkv_mu_lora[l], rwkv_w0[l], rwkv_w2[l],
                           rwkv_a0[l], rwkv_a2[l], rwkv_k_k[l], rwkv_k_a[l], rwkv_r_k[l],
                           rwkv_lnx_w[l], rwkv_lnx_b[l])
        gates = jax.nn.sigmoid(p_gate.reshape(b, s, N_BRANCHES, d) + gate_b[l])
        merged = (gates[:, :, 0] * (y_a @ w_branch_a[l])
                  + gates[:, :, 1] * (y_b @ w_branch_b[l])
                  + gates[:, :, 2] * (y_c @ w_branch_c[l]))
        out = (merged @ w_out[l]).astype(x.dtype)
        x = layer_norm(DEEPNORM_ALPHA * x + out, ln_g[l], ln_b[l])
    return x
```

```python
import contextlib
import os
DBG = int(os.environ.get('P1_DBG', '9'))
RDBG = int(os.environ.get('P1_RDBG', '9'))
import numpy as np
import concourse.bass as bass
import concourse.mybir as mybir
from concourse.bass_utils import run_bass_kernel_spmd

F32 = mybir.dt.float32
BF16 = mybir.dt.bfloat16
AF = mybir.ActivationFunctionType
ALU = mybir.AluOpType
AX = mybir.AxisListType

NCORES = 8
D = 2048
DEPTH = 4
ALPHA = (2 * DEPTH) ** 0.25
LN_EPS = 1e-5

ENGS = ("pe", "act", "dve", "pool", "sp")
NRING = 12


class Buf:
    __slots__ = ("name", "last_w", "reads")

    def __init__(self, name=""):
        self.name = name
        self.last_w = None
        self.reads = []


class Sched:
    def __init__(self, nc, same_engine_sync=True):
        self.nc = nc
        self.stack = contextlib.ExitStack()
        self.ops = {e: [] for e in ENGS}
        self.cnt = {e: 0 for e in ENGS}
        self.sems = {}
        for e in ("pe", "act", "dve", "pool"):
            self.sems[e] = self.stack.enter_context(nc.semaphore("s_" + e))
        self.ring_cnt = {}
        self.ring_pos = {}
        for q in ("sp", "pool", "act"):
            self.ring_cnt[q] = [0] * NRING
            self.ring_pos[q] = 0
            for i in range(NRING):
                self.sems[(q, i)] = self.stack.enter_context(nc.semaphore(f"r_{q}{i}"))
        self.waited = {e: {} for e in ENGS}
        self.same_engine_sync = same_engine_sync
        self.out_events = []
        self._n = 0

    def sbuf(self, name, shape, dtype):
        return self.stack.enter_context(self.nc.sbuf_tensor("sb_" + name, list(shape), dtype))

    def psum(self, name, shape, dtype):
        return self.stack.enter_context(self.nc.psum_tensor("ps_" + name, list(shape), dtype))

    def _deps(self, eng, R, W):
        need = {}

        def add(ev):
            k, v = ev
            if k == eng and (not self.same_engine_sync or eng == "pe"):
                return
            if v > need.get(k, 0):
                need[k] = v
        for b in R:
            if b.last_w is not None:
                add(b.last_w)
        for b in W:
            if b.last_w is not None:
                add(b.last_w)
            for ev in b.reads:
                add(ev)
        waits = []
        wd = self.waited[eng]
        for k, v in need.items():
            if wd.get(k, 0) >= v:
                continue
            wd[k] = v
            waits.append((k, v))
        return waits

    def _commit(self, ev, R, W):
        for b in R:
            b.reads.append(ev)
            if len(b.reads) > 48:
                mx = {}
                for (k, v) in b.reads:
                    if v > mx.get(k, 0):
                        mx[k] = v
                b.reads = list(mx.items())
        for b in W:
            b.last_w = ev
            b.reads = []

    def op(self, eng, fn, R=(), W=()):
        waits = self._deps(eng, R, W)
        self.cnt[eng] += 1
        ev = (eng, self.cnt[eng])
        self.ops[eng].append((waits, fn, None))
        self._commit(ev, R, W)
        return ev

    def pe(self, fn, R=(), W=()):
        return self.op("pe", fn, R, W)

    def act(self, fn, R=(), W=()):
        return self.op("act", fn, R, W)

    def dve(self, fn, R=(), W=()):
        return self.op("dve", fn, R, W)

    def pool(self, fn, R=(), W=()):
        return self.op("pool", fn, R, W)

    def dma(self, fn, R=(), W=(), q="sp", is_output=False):
        i = self.ring_pos[q]
        self.ring_pos[q] = (i + 1) % NRING
        key = (q, i)
        waits = self._deps(q, R, W)
        prev = self.ring_cnt[q][i]
        if prev > 0 and self.waited[q].get(key, 0) < prev:
            self.waited[q][key] = prev
            waits.append((key, prev))
        val = prev + 16
        self.ring_cnt[q][i] = val
        ev = (key, val)
        self.ops[q].append((waits, fn, ev))
        self._commit(ev, R, W)
        if is_output:
            self.out_events.append(ev)
        return ev

    def emit(self):
        nc = self.nc
        fin = {}
        for (k, v) in self.out_events:
            if v > fin.get(k, 0):
                fin[k] = v
        fin_waits = list(fin.items())

        def run(engname, e):
            sem_e = self.sems.get(engname)
            for (waits, fn, dma_ev) in self.ops[engname]:
                for (k, v) in waits:
                    e.wait_ge(self.sems[k], v)
                ins = fn(e)
                if dma_ev is not None:
                    ins.then_inc(self.sems[dma_ev[0]], 16)
                else:
                    ins.then_inc(sem_e, 1)
            if engname == "sp":
                for (k, v) in fin_waits:
                    e.wait_ge(self.sems[k], v)

        with nc.Block() as block:
            @block.tensor
            def _(e):
                run("pe", e)

            @block.scalar
            def _(e):
                run("act", e)

            @block.vector
            def _(e):
                run("dve", e)

            @block.gpsimd
            def _(e):
                run("pool", e)

            @block.sync
            def _(e):
                run("sp", e)

    def close(self):
        self.stack.close()


class Rot:
    def __init__(self, S, name, n, shape, dtype, psum=False):
        self.items = []
        for i in range(n):
            t = (S.psum if psum else S.sbuf)(f"{name}{i}", shape, dtype)
            self.items.append((t, Buf(f"{name}{i}")))
        self.i = 0

    def next(self):
        it = self.items[self.i]
        self.i = (self.i + 1) % len(self.items)
        return it


class WStream:
    def __init__(self, S, ktmax=16, nstage=3, nbf=5, cast_engs=("pool",)):
        self.S = S
        self.stage = Rot(S, "wst", nstage, [128, ktmax * 128], F32)
        self.bf = Rot(S, "wbf", nbf, [128, ktmax * 128], BF16)
        self.cast_engs = cast_engs
        self.ci = 0

    def get(self, dram_ap, kt, ncol=128):
        S = self.S
        st, bst = self.stage.next()
        wb, bwb = self.bf.next()
        n = kt * ncol
        S.dma(lambda e: e.dma_start(out=st[:, 0:n], in_=dram_ap), W=[bst])
        eng = self.cast_engs[self.ci % len(self.cast_engs)]
        self.ci += 1
        if eng == "act":
            S.act(lambda e: e.activation(out=wb[:, 0:n], in_=st[:, 0:n], func=AF.Copy), R=[bst], W=[bwb])
        else:
            S.op(eng, lambda e: e.tensor_copy(out=wb[:, 0:n], in_=st[:, 0:n]), R=[bst], W=[bwb])
        return wb[:, 0:n].rearrange("p (k c) -> p k c", k=kt), bwb


PV2_GATEB = 0
PV2_BGLU = 48
PV2_LNG = 56
PV2_LNB = 72
PV2_N = 88


def build_p2(NTc):
    TB = min(256, NTc)
    nblk = NTc // TB
    nc = bass.Bass("TRN2", target_bir_lowering=False)
    dt = lambda name, shape, kind="ExternalInput": nc.dram_tensor(name, list(shape), F32, kind=kind).ap()
    xT = dt("xT", [D, NTc]); yaT = dt("yaT", [2048, NTc]); gbT = dt("gbT", [1024, NTc]); ycT = dt("ycT", [1024, NTc])
    wg = dt("wg", [48, 128, 16 * 128]); wzb = dt("wzb", [8, 128, 16 * 128]); wglu = dt("wglu", [8, 128, 8 * 128])
    wba = dt("wba", [16, 128, 16 * 128]); wbb = dt("wbb", [16, 128, 8 * 128]); wbc = dt("wbc", [16, 128, 8 * 128])
    wo = dt("wo", [16, 128, 16 * 128])
    pvd = dt("pv", [128, PV2_N])
    xnT = dt("xnT", [D, NTc], kind="ExternalOutput")

    S = Sched(nc)
    ws = WStream(S, ktmax=16, nstage=3, nbf=6)
    pv = S.sbuf("pv", [128, PV2_N], F32); bpv = Buf("pv")
    ones = S.sbuf("ones", [128, 128], F32); bones = Buf("ones")
    S.dma(lambda e: e.dma_start(out=pv[:], in_=pvd), W=[bpv])
    S.dve(lambda e: e.memset(ones[:], 1.0), W=[bones])

    astage = Rot(S, "ast", 3, [128, 4, TB], F32)
    xb = S.sbuf("xb", [128, 16, TB], BF16); bxb = Buf("xb")
    yab = S.sbuf("yab", [128, 16, TB], BF16); byab = Buf("yab")
    gbb = S.sbuf("gbb", [128, 8, TB], BF16); bgbb = Buf("gbb")
    ycb = S.sbuf("ycb", [128, 8, TB], BF16); bycb = Buf("ycb")
    szb = S.sbuf("szb", [128, 8, TB], BF16); bszb = Buf("szb")
    ybb = S.sbuf("ybb", [128, 8, TB], BF16); bybb = Buf("ybb")
    mg = S.sbuf("mg", [128, 16, TB], BF16); bmg = Buf("mg")
    hT = S.sbuf("hT", [128, 16, TB], F32); bhT = Buf("hT")
    xres = Rot(S, "xres", 2, [128, TB], F32)
    sgt = Rot(S, "sgt", 4, [128, TB], BF16)
    tmpa = Rot(S, "tmpa", 3, [128, TB], F32)
    sqt = Rot(S, "sqt", 2, [128, TB], F32)
    meanb = S.sbuf("meanb", [128, TB], F32); bmean = Buf("meanb")
    rstdb = S.sbuf("rstdb", [128, TB], F32); brstd = Buf("rstdb")
    outt = Rot(S, "outt", 2, [128, TB], F32)
    ps = Rot(S, "ps", 7, [128, 512], F32, psum=True)
    pstat = S.psum("pstat", [128, 512], F32); bpstat = Buf("pstat")

    def load_cast(src, kt_total, dst, bdst, t0):
        for k0 in range(0, kt_total, 4):
            st, bst = astage.next()
            S.dma(lambda e, st=st, k0=k0: e.dma_start(
                out=st[:], in_=src.rearrange("(k p) t -> p k t", p=128)[:, k0:k0 + 4, t0:t0 + TB]), W=[bst])
            S.pool(lambda e, st=st, k0=k0: e.tensor_copy(out=dst[:, k0:k0 + 4, :], in_=st[:]), R=[bst], W=[bdst])

    def mm_group(pst, bps, wv, bw, act, bact, kt):
        for k in range(kt):
            S.pe(lambda e, k=k: e.matmul(pst[:, 0:TB], lhsT=wv[:, k, :], rhs=act[:, k, :],
                                          start=(k == 0), stop=(k == kt - 1)), R=[bw, bact], W=[bps])

    for blk in range(nblk):
        t0 = blk * TB
        load_cast(xT, 16, xb, bxb, t0)
        load_cast(gbT, 8, gbb, bgbb, t0)
        load_cast(yaT, 16, yab, byab, t0)
        load_cast(ycT, 8, ycb, bycb, t0)
        for m in range(8):
            wv, bw = ws.get(wzb[m], 16)
            pst, bps = ps.next()
            mm_group(pst, bps, wv, bw, xb, bxb, 16)
            S.act(lambda e, pst=pst, m=m: e.activation(out=szb[:, m, :], in_=pst[:, 0:TB], func=AF.Silu), R=[bps], W=[bszb])
        for m in range(8):
            wv, bw = ws.get(wglu[m], 8)
            pst, bps = ps.next()
            mm_group(pst, bps, wv, bw, gbb, bgbb, 8)
            sg, bsg = sgt.next()
            S.act(lambda e, pst=pst, m=m, sg=sg: e.activation(out=sg[:], in_=pst[:, 0:TB], func=AF.Sigmoid,
                                                            bias=pv[:, PV2_BGLU + m:PV2_BGLU + m + 1]), R=[bps, bpv], W=[bsg])
            S.dve(lambda e, m=m, sg=sg: e.tensor_tensor(out=sg[:], in0=sg[:], in1=gbb[:, m, :], op=ALU.mult), R=[bsg, bgbb], W=[bsg])
            S.dve(lambda e, m=m, sg=sg: e.tensor_tensor(out=ybb[:, m, :], in0=sg[:], in1=szb[:, m, :], op=ALU.mult), R=[bsg, bszb], W=[bybb])
        for j in range(16):
            gates = []
            for br in range(3):
                wv, bw = ws.get(wg[br * 16 + j], 16)
                pst, bps = ps.next()
                mm_group(pst, bps, wv, bw, xb, bxb, 16)
                sg, bsg = sgt.next()
                col = PV2_GATEB + br * 16 + j
                S.act(lambda e, pst=pst, sg=sg, col=col: e.activation(out=sg[:], in_=pst[:, 0:TB], func=AF.Sigmoid,
                                                                    bias=pv[:, col:col + 1]), R=[bps, bpv], W=[bsg])
                gates.append((sg, bsg))
            prods = []
            for br, (wd, kt, a, ba) in enumerate(((wba, 16, yab, byab), (wbb, 8, ybb, bybb), (wbc, 8, ycb, bycb))):
                wv, bw = ws.get(wd[j], kt)
                pst, bps = ps.next()
                mm_group(pst, bps, wv, bw, a, ba, kt)
                sg, bsg = gates[br]
                tm, btm = tmpa.next()
                S.dve(lambda e, pst=pst, sg=sg, tm=tm: e.tensor_tensor(out=tm[:], in0=pst[:, 0:TB], in1=sg[:], op=ALU.mult),
                      R=[bps, bsg], W=[btm])
                prods.append((tm, btm))
            (p0, b0), (p1, b1), (p2, b2) = prods
            S.dve(lambda e, p0=p0, p1=p1: e.tensor_tensor(out=p0[:], in0=p0[:], in1=p1[:], op=ALU.add), R=[b0, b1], W=[b0])
            S.dve(lambda e, p0=p0, p2=p2, j=j: e.tensor_tensor(out=mg[:, j, :], in0=p0[:], in1=p2[:], op=ALU.add), R=[b0, b2], W=[bmg])
        for j in range(16):
            wv, bw = ws.get(wo[j], 16)
            pst, bps = ps.next()
            mm_group(pst, bps, wv, bw, mg, bmg, 16)
            xr, bxr = xres.next()
            S.dma(lambda e, xr=xr, j=j, t0=t0: e.dma_start(out=xr[:], in_=xT[j * 128:(j + 1) * 128, t0:t0 + TB]), W=[bxr])
            S.dve(lambda e, xr=xr, pst=pst, j=j: e.scalar_tensor_tensor(out=hT[:, j, :], in0=xr[:], scalar=float(ALPHA), in1=pst[:, 0:TB],
                                                                     op0=ALU.mult, op1=ALU.add), R=[bxr, bps], W=[bhT])
        for j in range(16):
            S.pe(lambda e, j=j: e.matmul(pstat[:, 0:TB], lhsT=ones[:], rhs=hT[:, j, :], start=(j == 0), stop=(j == 15)),
                 R=[bones, bhT], W=[bpstat])
        S.act(lambda e: e.activation(out=meanb[:], in_=pstat[:, 0:TB], func=AF.Copy, scale=1.0 / D), R=[bpstat], W=[bmean])
        for j in range(16):
            S.dve(lambda e, j=j: e.tensor_tensor(out=hT[:, j, :], in0=hT[:, j, :], in1=meanb[:], op=ALU.subtract), R=[bhT, bmean], W=[bhT])
        sqs = []
        for j in range(16):
            sq, bsq = sqt.next()
            S.act(lambda e, j=j, sq=sq: e.activation(out=sq[:], in_=hT[:, j, :], func=AF.Square), R=[bhT], W=[bsq])
            S.pe(lambda e, j=j, sq=sq: e.matmul(pstat[:, 0:TB], lhsT=ones[:], rhs=sq[:], start=(j == 0), stop=(j == 15)),
                 R=[bones, bsq], W=[bpstat])
        S.act(lambda e: e.activation(out=rstdb[:], in_=pstat[:, 0:TB], func=AF.Sqrt, scale=1.0 / D, bias=float(LN_EPS)), R=[bpstat], W=[brstd])
        S.dve(lambda e: e.reciprocal(out=rstdb[:], in_=rstdb[:]), R=[brstd], W=[brstd])
        for j in range(16):
            S.dve(lambda e, j=j: e.tensor_tensor(out=hT[:, j, :], in0=hT[:, j, :], in1=rstdb[:], op=ALU.mult), R=[bhT, brstd], W=[bhT])
            ot, bot = outt.next()
            S.act(lambda e, j=j, ot=ot: e.activation(out=ot[:], in_=hT[:, j, :], func=AF.Identity,
                                                    scale=pv[:, PV2_LNG + j:PV2_LNG + j + 1], bias=pv[:, PV2_LNB + j:PV2_LNB + j + 1]),
                  R=[bhT, bpv], W=[bot])
            S.dma(lambda e, j=j, ot=ot, t0=t0: e.dma_start(out=xnT[j * 128:(j + 1) * 128, t0:t0 + TB], in_=ot[:]), R=[bot], q="act", is_output=True)
    S.emit()
    S.close()
    return nc


def tile_w(W):
    K, N = W.shape
    kt, ct = K // 128, N // 128
    return np.ascontiguousarray(W.reshape(kt, 128, ct, 128).transpose(2, 1, 0, 3).reshape(ct, 128, kt * 128))


def pcol(v):
    return np.ascontiguousarray(np.asarray(v, np.float32).reshape(-1, 128).T)


(PC_CONVW, PC_CONVB, PC_NORMW, PC_DS5, PC_S5P, PC_MU, PC_MULORA, PC_W0, PC_A0, PC_KK, PC_KA, PC_RK,
 PC_LNW, PC_LNB) = (0, 16, 20, 22, 23, 35, 39, 40, 41, 42, 43, 44, 45, 46)
PV1_N = 47
BT_DTB, BT_ALOG, BT_D = 0, 4, 8
BT1_N = 264
CS_ID, CS_LE, CS_LT, CS_GT, CS_BO = 0, 128, 256, 384, 512
CST_N = 640
NCM = 10
RW_EPS = 64e-5
SSD_EPS = 1e-5
EXPM05 = float(np.exp(-0.5))
PI = float(np.pi)


def build_p1(NT, S_len, TB, x_bf16, parts=("s5", "ssd", "rwkv")):
    assert S_len % TB == 0 and TB % 128 == 0
    nblk = NT // TB
    nch = TB // 128
    TS = min(256, TB)
    nc = bass.Bass("TRN2", target_bir_lowering=False)
    din = lambda name, shape, dty=F32: nc.dram_tensor(name, list(shape), dty, kind="ExternalInput").ap()
    xT = din("xT", [D, NT], BF16 if x_bf16 else F32)
    w1cm = din("w1cm", [NCM, 128, 16 * 128])
    w1tm = din("w1tm", [128, 16 * 260])
    pvd = din("pv", [128, PV1_N]); btd = din("bt", [128, BT1_N]); cstd = din("cst", [128, CST_N])
    w2a2d = din("w2a2", [128, 128])
    s5bd = din("s5b", [128, 8 * 128])
    s5cd = din("s5c", [128, 8 * 128])
    yaT = nc.dram_tensor("yaT", [256, NT], BF16, kind="ExternalOutput").ap()
    gbT = nc.dram_tensor("gbT", [128, NT], BF16, kind="ExternalOutput").ap()
    ycT = nc.dram_tensor("ycT", [128, NT], BF16, kind="ExternalOutput").ap()

    S = Sched(nc)
    B = Buf

    def TT(eng, out, a, b, op, R, W):
        return S.op(eng, lambda e: e.tensor_tensor(out=out, in0=a, in1=b, op=op), R, W)

    def TSC(eng, out, a, s1, s2, op0, op1, R, W):
        if op1 is None:
            return S.op(eng, lambda e: e.tensor_scalar(out=out, in0=a, scalar1=s1, scalar2=None, op0=op0), R, W)
        return S.op(eng, lambda e: e.tensor_scalar(out=out, in0=a, scalar1=s1, scalar2=s2, op0=op0, op1=op1), R, W)

    def STT(out, a, s, b, op0, op1, R, W):
        return S.dve(lambda e: e.scalar_tensor_tensor(out=out, in0=a, scalar=s, in1=b, op0=op0, op1=op1), R, W)

    def ACT(out, a, func, R, W, **kw):
        return S.act(lambda e: e.activation(out=out, in_=a, func=func, **kw), R, W)

    def CP(eng, out, a, R, W):
        if eng == "act":
            return ACT(out, a, AF.Copy, R, W)
        return S.op(eng, lambda e: e.tensor_copy(out=out, in_=a), R, W)

    def MM(out, lhsT, rhs, start, stop, R, W):
        return S.pe(lambda e: e.matmul(out, lhsT=lhsT, rhs=rhs, start=start, stop=stop), R, W)

    def TR(out, a, ident, R, W):
        return S.pe(lambda e: e.matmul(out, lhsT=a, rhs=ident, start=True, stop=True), R, W)

    pv = S.sbuf("pv", [128, PV1_N], F32); bpv = B()
    omv = S.sbuf("omv", [128, PV1_N], F32)
    bt = S.sbuf("bt", [128, BT1_N], F32); bbt = B()
    cst = S.sbuf("cst", [128, CST_N], F32); bcst = B()
    cstb = S.sbuf("cstb", [128, CST_N], BF16)
    ones = S.sbuf("ones", [128, 256], F32); bones = B()
    m4 = S.sbuf("m4", [128, 512], F32)
    w2a2s = S.sbuf("w2a2s", [128, 128], F32); w2a2 = S.sbuf("w2a2", [128, 128], BF16); bw2 = B()
    s5bs = S.sbuf("s5bs", [128, 1024], F32); s5b = S.sbuf("s5b", [128, 1024], BF16); bs5b = B()
    s5c = S.sbuf("s5c", [128, 1024], F32); bs5c = B()
    S.dma(lambda e: e.dma_start(out=pv[:], in_=pvd), W=[bpv])
    S.dma(lambda e: e.dma_start(out=bt[:], in_=btd), W=[bbt])
    S.dma(lambda e: e.dma_start(out=cst[:], in_=cstd), W=[bcst])
    S.dma(lambda e: e.dma_start(out=w2a2s[:], in_=w2a2d), W=[bw2])
    S.dma(lambda e: e.dma_start(out=s5bs[:], in_=s5bd), W=[bs5b])
    S.dma(lambda e: e.dma_start(out=s5c[:], in_=s5cd), W=[bs5c])
    TSC("dve", omv[:], pv[:], -1.0, 1.0, ALU.mult, ALU.add, [bpv], [bpv])
    CP("dve", cstb[:], cst[:], [bcst], [bcst])
    S.dve(lambda e: e.memset(ones[:], 1.0), W=[bones])
    for i, src in enumerate((CS_LT, CS_LE, CS_LT, CS_LE)):
        CP("dve", m4[:, i * 128:(i + 1) * 128], cst[:, src:src + 128], [bcst], [bcst])
    CP("dve", w2a2[:], w2a2s[:], [bw2], [bw2])
    CP("dve", s5b[:], s5bs[:], [bs5b], [bs5b])
    TSC("dve", s5c[:, 512:1024], s5c[:, 512:1024], -1.0, None, ALU.mult, None, [bs5c], [bs5c])
    ACT(bt[:, BT_ALOG:BT_ALOG + 4], bt[:, BT_ALOG:BT_ALOG + 4], AF.Exp, [bbt], [bbt])
    TSC("dve", bt[:, BT_ALOG:BT_ALOG + 4], bt[:, BT_ALOG:BT_ALOG + 4], -1.0, None, ALU.mult, None, [bbt], [bbt])
    ident = cstb[:, CS_ID:CS_ID + 128]
    bones_bf = cstb[:, CS_BO:CS_BO + 128]
    mle = cst[:, CS_LE:CS_LE + 128]
    mgt = cst[:, CS_GT:CS_GT + 128]
    pcol = lambda c: pv[:, c:c + 1]

    wcm = S.sbuf("wcm", [128, NCM, 16 * 128], BF16); bwcm = B()
    wtm = S.sbuf("wtm", [128, 16 * 260], BF16); bwtm = B()
    wstage = Rot(S, "wstg", 2, [128, 2048], F32)
    for i in range(NCM):
        st, bst = wstage.next()
        S.dma(lambda e, st=st, i=i: e.dma_start(out=st[:, 0:2048], in_=w1cm[i]), W=[bst])
        CP("pool", wcm[:, i, :], st[:, 0:2048], [bst], [bwcm])
    for c0 in range(0, 16 * 260, 2048):
        c1 = min(c0 + 2048, 16 * 260)
        st, bst = wstage.next()
        S.dma(lambda e, st=st, c0=c0, c1=c1: e.dma_start(out=st[:, 0:c1 - c0], in_=w1tm[:, c0:c1]), W=[bst])
        CP("pool", wtm[:, c0:c1], st[:, 0:c1 - c0], [bst], [bwtm])
    wcmv = wcm[:].rearrange("p i (k c) -> p i k c", k=16)
    wtmv = wtm[:].rearrange("p (k c) -> p k c", k=16)

    pg = Rot(S, "pg", 2, [128, 512], F32, psum=True)
    pA, bpA = S.psum("pA", [128, 512], F32), B()
    pI, bpI = S.psum("pI", [128, 512], F32), [B()] * 4
    pS, bpS = S.psum("pS", [128, 512], F32), [B()] * 8
    pT, bpT = S.psum("pT", [128, 512], F32), [B()] * 4
    pD, bpD = S.psum("pD", [128, 512], F32), [B()] * 3
    pX, bpX = S.psum("pX", [128, 512], F32), B()
    pT_i = [0]

    def tslot():
        if os.environ.get("P1_TSLOT", "1") == "1":
            t_, b_ = pg.next()
            return t_[:, 0:128], b_
        i = pT_i[0]
        pT_i[0] = (i + 1) % 4
        return pT[:, i * 128:i * 128 + 128], bpT[i]

    tbl = S.sbuf("s5tbl", [128, 4, 5, TS], F32); btbl = [B() for _ in range(4)]
    sc = S.sbuf("s5sc", [128, 4, 24], F32)
    tmpw = S.sbuf("s5tmpw", [128, TS], F32)
    pic = S.sbuf("pic", [128, 2], F32)
    S.dve(lambda e: e.memset(pic[:], PI), W=[bpv])
    for gp in range(4):
        bb = [bpv, btbl[gp]]
        c0 = PC_S5P + 3 * gp
        lre, lim, ldt = pcol(c0), pcol(c0 + 1), pcol(c0 + 2)
        s = lambda j, gp=gp: sc[:, gp, j:j + 1]
        ACT(s(0), ldt, AF.Exp, bb, bb)
        TT("dve", s(1), lim, s(0), ALU.mult, bb, bb)
        S.act(lambda e, gp=gp, lre=lre: e.activation(out=sc[:, gp, 2:3], in_=lre, func=AF.Exp, scale=sc[:, gp, 0:1]), bb, bb)
        TSC("dve", s(3), s(1), 0.0, None, ALU.add, None, bb, bb)
        TSC("dve", s(4), s(1), PI / 2, None, ALU.add, None, bb, bb)
        for j in (3, 4):
            for _ in range(4):
                TT("dve", s(5), s(j), pic[:, 0:1], ALU.is_gt, bb, bb)
                STT(s(j), s(5), -2.0 * PI, s(j), ALU.mult, ALU.add, bb, bb)
        Oc, Os = tbl[:, gp, 2, :], tbl[:, gp, 3, :]
        ACT(Os[:, 0:1], s(3), AF.Sin, bb, bb)
        ACT(Oc[:, 0:1], s(4), AF.Sin, bb, bb)
        w = 1
        while w < TS:
            cw, sw = Oc[:, w - 1:w], Os[:, w - 1:w]
            TSC("dve", tmpw[:, 0:w], Os[:, 0:w], sw, None, ALU.mult, None, bb, bb)
            STT(Oc[:, w:2 * w], Oc[:, 0:w], cw, tmpw[:, 0:w], ALU.mult, ALU.subtract, bb, bb)
            TSC("dve", tmpw[:, 0:w], Os[:, 0:w], cw, None, ALU.mult, None, bb, bb)
            STT(Os[:, w:2 * w], Oc[:, 0:w], sw, tmpw[:, 0:w], ALU.mult, ALU.add, bb, bb)
            w *= 2
        TT("dve", s(6), s(2), Oc[:, 0:1], ALU.mult, bb, bb)
        TT("dve", s(7), s(2), Os[:, 0:1], ALU.mult, bb, bb)
        TSC("dve", s(8), s(6), -1.0, None, ALU.add, None, bb, bb)
        TT("dve", s(9), lre, lre, ALU.mult, bb, bb)
        STT(s(9), lim, lim, s(9), ALU.mult, ALU.add, bb, bb)
        S.dve(lambda e, gp=gp: e.reciprocal(out=sc[:, gp, 10:11], in_=sc[:, gp, 9:10]), bb, bb)
        TT("dve", s(11), s(8), lre, ALU.mult, bb, bb)
        STT(s(11), s(7), lim, s(11), ALU.mult, ALU.add, bb, bb)
        TT("dve", s(12), s(11), s(10), ALU.mult, bb, bb)
        TT("dve", s(13), s(8), lim, ALU.mult, bb, bb)
        STT(s(13), s(7), lre, s(13), ALU.mult, ALU.subtract, bb, bb)
        TT("dve", s(14), s(13), s(10), ALU.mult, bb, bb)
        TSC("dve", s(15), s(12), -1.0, None, ALU.mult, None, bb, bb)
        Ere, Eim, rho = tbl[:, gp, 0, :], tbl[:, gp, 1, :], tbl[:, gp, 4, :]
        TSC("dve", tmpw[:], Os, s(14), None, ALU.mult, None, bb, bb)
        STT(Ere, Oc, s(12), tmpw[:], ALU.mult, ALU.add, bb, bb)
        TSC("dve", tmpw[:], Os, s(15), None, ALU.mult, None, bb, bb)
        STT(Eim, Oc, s(14), tmpw[:], ALU.mult, ALU.add, bb, bb)
        TSC("dve", rho, ones[:, 0:TS], s(2), None, ALU.mult, None, bb + [bones], bb)

    hst = S.sbuf("s5h", [128, 4, 2], F32); bhst = [B() for _ in range(4)]
    SS = S.sbuf("ssd_state", [128, 4, 64], F32); SSb = S.sbuf("ssd_state_b", [128, 4, 64], BF16); bSS = [B() for _ in range(4)]
    ST = S.sbuf("rw_state", [128, 64], F32); STb = S.sbuf("rw_state_b", [128, 64], BF16); bST = [B(), B()]

    xstage = Rot(S, "xstg", 2, [128, 4, TB], F32) if not x_bf16 else None
    xbR = Rot(S, "xb", 2, [128, 16, TB], BF16)
    Uc = [S.sbuf(f"uc{i}", [128, TB + 3], F32) for i in range(4)]; bUc = [B() for _ in range(4)]
    Rw = [S.sbuf(f"rw{i}", [128, TB + 1], F32) for i in range(5)]; bRw = [B() for _ in range(5)]
    u_t = S.sbuf("u_t", [128, TB], BF16); bu = B()
    f32t = lambda name, n=TB: (S.sbuf(name, [128, n], F32), B())
    bf16t = lambda name, n=TB: (S.sbuf(name, [128, n], BF16), B())
    cacc, bcacc = f32t("cacc")
    xsc = [bf16t("xsc0"), bf16t("xsc1")]
    Bc, bBc = bf16t("Bc"); Cc, bCc = bf16t("Cc")
    xtok, bxtok = bf16t("xtok", 256)
    Btok, bBtok = bf16t("Btok", 128)
    sm = S.sbuf("ssd_small", [128, 64], F32); bsm = B()
    CBm, bCBm = f32t("CBm", 128)
    Dh, bDh = f32t("Dh", 128)
    argt, bargt = f32t("argt", 128)
    Gt, bGt = bf16t("Gt", 128)
    ecs, becs = f32t("ecs", 128)
    Csc, bCsc = bf16t("Csc", 128)
    Xdt, bXdt = bf16t("Xdt", 64); Xdec, bXdec = bf16t("Xdec", 64)
    ye, bye = f32t("ye", 256); sza, bsza = f32t("sza", 256); junk, bjunk = f32t("junk", 256)
    ynb, bynb = bf16t("ynb", 256)
    ya_out = S.sbuf("ya_out", [128, 2, TB], BF16); bya_out = B()
    s5t = [f32t(f"s5t{i}", TS) for i in range(4)]
    s5x = [f32t("s5xr", TS), f32t("s5xi", TS)]
    s5q = [f32t("s5qr", TS), f32t("s5qi", TS)]
    s5h = [[f32t(f"s5hr{g}", TS), f32t(f"s5hi{g}", TS)] for g in range(2)]
    s5y, bs5y = f32t("s5y", TS); s5y2, bs5y2 = f32t("s5y2", TS)
    gb_out, bgb_out = bf16t("gb_out")
    rwm = [f32t(f"rwm{i}") for i in range(5)]
    tmpR, btmpR = f32t("tmpR")
    lw, blw = bf16t("lw")
    ld, bld = f32t("ld"); asig, basig = f32t("asig"); kk, bkk = f32t("kk"); kk2, bkk2 = bf16t("kk2")
    rs, brs = f32t("rs"); kkn, bkkn = f32t("kkn"); kp, bkp = f32t("kp"); bvec, bbvec = f32t("bvec")
    prod, bprod = bf16t("prod"); rv, brv = f32t("rv"); sz, bsz = f32t("sz")
    LW, bLW = f32t("LW"); Ep, bEp = f32t("Ep"); Em, bEm = f32t("Em"); Epr, bEpr = f32t("Epr"); Ee, bEe = f32t("Ee")
    AR = S.sbuf("AR", [128, nch, 2, 128], BF16); bAR = B()
    BK = S.sbuf("BK", [128, nch, 2, 128], BF16); bBK = B()
    bh, bbh = bf16t("bh"); kh, bkh = bf16t("kh"); vb, bvb = bf16t("vb")
    vT, bvT = bf16t("vT", 128); bhT, bbhT = bf16t("bhT", 128); khT, bkhT = bf16t("khT", 128)
    M4m = [bf16t("M4m0", 512), bf16t("M4m1", 512)]
    Pq = [[bf16t(f"P{h}{i}", 128) for i in range(2)] for h in range(2)]
    Qq = [[bf16t(f"Q{h}{i}", 128) for i in range(2)] for h in range(2)]
    Rq = [[bf16t(f"R{h}{i}", 128) for i in range(2)] for h in range(2)]
    Zs = [bf16t("Zs0", 64), bf16t("Zs1", 64)]; Us = [bf16t("Us0", 64), bf16t("Us1", 64)]
    st6, bst6 = f32t("st6", 16)
    ynr, bynr = bf16t("ynr", 128)
    yo, byo = f32t("yo", 128)
    yc_out, byc_out = bf16t("yc_out")

    for blk in range(nblk):
        t0 = blk * TB
        first = (t0 % S_len == 0)
        xb, bxb = xbR.next()
        xsrc = xT.rearrange("(k p) t -> p k t", p=128)
        if x_bf16:
            for k0 in range(0, 16, 8):
                S.dma(lambda e, k0=k0, xb=xb, t0=t0: e.dma_start(out=xb[:, k0:k0 + 8, :], in_=xsrc[:, k0:k0 + 8, t0:t0 + TB]), W=[bxb])
        else:
            for k0 in range(0, 16, 4):
                st, bst = xstage.next()
                S.dma(lambda e, k0=k0, st=st, t0=t0: e.dma_start(out=st[:], in_=xsrc[:, k0:k0 + 4, t0:t0 + TB]), W=[bst])
                CP("pool", xb[:, k0:k0 + 4, :], st[:], [bst], [bxb])
        if first:
            for i in range(4):
                S.dve(lambda e, i=i: e.memset(Uc[i][:, 0:3], 0.0), W=[bUc[i]])
            for i in range(5):
                S.dve(lambda e, i=i: e.memset(Rw[i][:, 0:1], 0.0), W=[bRw[i]])
            for g in range(4):
                S.dve(lambda e, g=g: e.memset(hst[:, g, :], 0.0), W=[bhst[g]])
                S.dve(lambda e, g=g: e.memset(SS[:, g, :], 0.0), W=[bSS[g]])
                S.dve(lambda e, g=g: e.memset(SSb[:, g, :], 0.0), W=[bSS[g]])
            S.dve(lambda e: e.memset(ST[:], 0.0), W=bST)
            S.dve(lambda e: e.memset(STb[:], 0.0), W=bST)
        else:
            for i in range(4):
                CP("dve", Uc[i][:, 0:3], Uc[i][:, TB:TB + 3], [bUc[i]], [bUc[i]])
            for i in range(5):
                CP("dve", Rw[i][:, 0:1], Rw[i][:, TB:TB + 1], [bRw[i]], [bRw[i]])
        for i in range(NCM):
            pst, bps = pg.next()
            for k in range(16):
                MM(pst[:, 0:TB], wcmv[:, i, k, :], xb[:, k, :], k == 0, k == 15, [bwcm, bxb], [bps])
            if i < 4:
                CP("act", Uc[i][:, 3:3 + TB], pst[:, 0:TB], [bps], [bUc[i]])
            elif i == 4:
                CP("act", u_t[:], pst[:, 0:TB], [bps], [bu])
            else:
                CP("act", Rw[i - 5][:, 1:1 + TB], pst[:, 0:TB], [bps], [bRw[i - 5]])

        for sub in (range(TB // TS) if "s5" in parts else ()):
            o = sub * TS
            for gp in range(4):
                MM(pX[:, 0:TS], s5b[:, gp * 128:(gp + 1) * 128], u_t[:, o:o + TS], True, True, [bs5b, bu], [bpX])
                MM(pX[:, 256:256 + TS], s5b[:, (4 + gp) * 128:(5 + gp) * 128], u_t[:, o:o + TS], True, True, [bs5b, bu], [bpX])
                Ere, Eim, Oc, Os, rho = (tbl[:, gp, j, :] for j in range(5))
                xre, xim = pX[:, 0:TS], pX[:, 256:256 + TS]
                (t1, b1), (t2, b2), (t3, b3), (t4, b4) = s5t
                tb_ = [btbl[gp]]
                TT("dve", t1[:], xre, Ere, ALU.mult, [bpX] + tb_, [b1])
                TT("dve", t2[:], xim, Eim, ALU.mult, [bpX] + tb_, [b2])
                TT("dve", t3[:], xre, Eim, ALU.mult, [bpX] + tb_, [b3])
                TT("dve", t4[:], xim, Ere, ALU.mult, [bpX] + tb_, [b4])
                (xr, bxr), (xi, bxi) = s5x
                TT("pool", xr[:], t1[:], t2[:], ALU.subtract, [b1, b2], [bxr])
                TT("pool", xi[:], t3[:], t4[:], ALU.add, [b3, b4], [bxi])
                (qr, bqr), (qi, bqi) = s5q
                S.dve(lambda e, qr=qr, xr=xr, rho=rho, gp=gp: e.tensor_tensor_scan(
                    out=qr[:], data0=rho, data1=xr[:], initial=hst[:, gp, 0:1], op0=ALU.mult, op1=ALU.add), [bxr, bhst[gp]] + tb_, [bqr])
                S.dve(lambda e, qi=qi, xi=xi, rho=rho, gp=gp: e.tensor_tensor_scan(
                    out=qi[:], data0=rho, data1=xi[:], initial=hst[:, gp, 1:2], op0=ALU.mult, op1=ALU.add), [bxi, bhst[gp]] + tb_, [bqi])
                (hr, bhr), (hi, bhi) = s5h[gp % 2]
                TT("dve", t1[:], qr[:], Oc, ALU.mult, [bqr] + tb_, [b1])
                TT("dve", t2[:], qi[:], Os, ALU.mult, [bqi] + tb_, [b2])
                TT("dve", t3[:], qr[:], Os, ALU.mult, [bqr] + tb_, [b3])
                TT("dve", t4[:], qi[:], Oc, ALU.mult, [bqi] + tb_, [b4])
                TT("pool", hr[:], t1[:], t2[:], ALU.subtract, [b1, b2], [bhr])
                TT("pool", hi[:], t3[:], t4[:], ALU.add, [b3, b4], [bhi])
                CP("dve", hst[:, gp, 0:1], hr[:, TS - 1:TS], [bhr], [bhst[gp]])
                CP("dve", hst[:, gp, 1:2], hi[:, TS - 1:TS], [bhi], [bhst[gp]])
                MM(pD[:, 256:256 + TS], s5c[:, gp * 128:(gp + 1) * 128], hr[:], gp == 0, False, [bs5c, bhr], [bpD[2]])
                MM(pD[:, 256:256 + TS], s5c[:, (4 + gp) * 128:(5 + gp) * 128], hi[:], False, gp == 3, [bs5c, bhi], [bpD[2]])
            STT(s5y[:], u_t[:, o:o + TS], pcol(PC_DS5), pD[:, 256:256 + TS], ALU.mult, ALU.add, [bu, bpv, bpD[2]], [bs5y])
            TT("dve", s5y2[:], s5y[:], s5y[:], ALU.mult, [bs5y], [bs5y2])
            TSC("dve", s5y2[:], s5y2[:], 0.044715, 1.0, ALU.mult, ALU.add, [bs5y2], [bs5y2])
            TT("dve", s5y2[:], s5y2[:], s5y[:], ALU.mult, [bs5y2, bs5y], [bs5y2])
            ACT(s5y2[:], s5y2[:], AF.Sigmoid, [bs5y2], [bs5y2], scale=1.5957691216057308)
            TT("dve", gb_out[:, o:o + TS], s5y[:], s5y2[:], ALU.mult, [bs5y, bs5y2], [bgb_out])
        if "s5" in parts:
            S.dma(lambda e, t0=t0: e.dma_start(out=gbT[:, t0:t0 + TB], in_=gb_out[:]), R=[bgb_out], q="act", is_output=True)
        if "ssd" not in parts and "rwkv" not in parts:
            continue

        if "ssd" in parts:
            conv_out = [xsc[0], xsc[1], (Bc, bBc), (Cc, bCc)]
            for i in range(4):
                wc = lambda k, i=i: pcol(PC_CONVW + 4 * i + k)
                TSC("dve", cacc[:], Uc[i][:, 0:TB], wc(0), pcol(PC_CONVB + i), ALU.mult, ALU.add, [bUc[i], bpv], [bcacc])
                for k in (1, 2, 3):
                    STT(cacc[:], Uc[i][:, k:k + TB], wc(k), cacc[:], ALU.mult, ALU.add, [bUc[i], bpv, bcacc], [bcacc])
                ACT(conv_out[i][0][:], cacc[:], AF.Silu, [bcacc], [conv_out[i][1]])
            for c in range(nch):
                o = c * 128
                sl = slice(o, o + 128)
                for i in range(2):
                    ts_, bts = tslot()
                    TR(ts_, xsc[i][0][:, sl], ident, [xsc[i][1], bcst], [bts])
                    CP("act", xtok[:, i * 128:(i + 1) * 128], ts_, [bts], [bxtok])
                if DBG < 2:
                    continue
                ts_, bts = tslot()
                TR(ts_, Bc[:, sl], ident, [bBc, bcst], [bts])
                CP("act", Btok[:], ts_, [bts], [bBtok])
                if DBG < 3:
                    continue
                pz, bpz = pg.next()
                for k in range(16):
                    MM(pz[:, 0:260], xb[:, k, sl], wtmv[:, k, :], k == 0, k == 15, [bxb, bwtm], [bpz])
                xd, ab, ee, ll, dtv, da, cs_s, csL, eL, decw = (sm[:, j * 4:(j + 1) * 4] for j in range(10))
                TT("dve", xd, pz[:, 256:260], bt[:, BT_DTB:BT_DTB + 4], ALU.add, [bpz, bbt], [bsm])
                STT(ab, xd, -1.0, xd, ALU.mult, ALU.max, [bsm], [bsm])
                ACT(ee, ab, AF.Exp, [bsm], [bsm], scale=-1.0)
                ACT(ll, ee, AF.Ln, [bsm], [bsm], bias=1.0)
                STT(dtv, xd, 0.0, ll, ALU.max, ALU.add, [bsm], [bsm])
                TT("dve", da, dtv, bt[:, BT_ALOG:BT_ALOG + 4], ALU.mult, [bsm, bbt], [bsm])
                MM(pS[:, 256:260], mle, da, True, True, [bcst, bsm], [bpS[4]])
                CP("act", cs_s, pS[:, 256:260], [bpS[4]], [bsm])
                if DBG < 4:
                    continue
                MM(pD[:, 0:128], Bc[:, sl], Cc[:, sl], True, True, [bBc, bCc], [bpD[0]])
                TT("dve", CBm[:], pD[:, 0:128], mle, ALU.mult, [bpD[0], bcst], [bCBm])
                for h in range(4):
                    hc = slice(h * 64, (h + 1) * 64)
                    TSC("dve", Dh[:], mle, da[:, h:h + 1], None, ALU.mult, None, [bcst, bsm], [bDh])
                    MM(pD[:, 128:256], ones[:, 0:128], Dh[:], True, True, [bones, bDh], [bpD[1]])
                    TSC("dve", argt[:], pD[:, 128:256], cs_s[:, h:h + 1], 0.0, ALU.subtract, ALU.min, [bpD[1], bsm], [bargt])
                    ACT(argt[:], argt[:], AF.Exp, [bargt], [bargt])
                    TT("dve", Gt[:], argt[:], CBm[:], ALU.mult, [bargt, bCBm], [bGt])
                    ACT(ecs[:], pD[:, 128:256], AF.Exp, [bpD[1]], [becs])
                    TT("pool", Csc[:], Cc[:, sl], ecs[:], ALU.mult, [bCc, becs], [bCsc])
                    CP("dve", csL[:, h:h + 1], pD[:, 255:256], [bpD[1]], [bsm])
                    ACT(eL[:, h:h + 1], csL[:, h:h + 1], AF.Exp, [bsm], [bsm])
                    ACT(decw[:, h:h + 1], cs_s[:, h:h + 1], AF.Exp, [bsm], [bsm], scale=-1.0, bias=csL[:, h:h + 1])
                    TSC("dve", Xdt[:], xtok[:, hc], dtv[:, h:h + 1], None, ALU.mult, None, [bxtok, bsm], [bXdt])
                    TSC("dve", Xdec[:], Xdt[:], decw[:, h:h + 1], None, ALU.mult, None, [bXdt, bsm], [bXdec])
                    MM(pD[:, 256 + h * 64:256 + (h + 1) * 64], Gt[:], Xdt[:], True, False, [bGt, bXdt], [bpD[2]])
                    MM(pD[:, 256 + h * 64:256 + (h + 1) * 64], Csc[:], SSb[:, h, :], False, True, [bCsc, bSS[h]], [bpD[2]])
                    MM(pS[:, 264:328], Btok[:], Xdec[:], True, True, [bBtok, bXdec], [bpS[5]])
                    STT(SS[:, h, :], SS[:, h, :], eL[:, h:h + 1], pS[:, 264:328], ALU.mult, ALU.add, [bsm, bpS[5], bSS[h]], [bSS[h]])
                    CP("act", SSb[:, h, :], SS[:, h, :], [bSS[h]], [bSS[h]])
                if DBG < 5:
                    continue
                TT("dve", ye[:], xtok[:], bt[:, BT_D:BT_D + 256], ALU.mult, [bxtok, bbt], [bye])
                TT("dve", ye[:], ye[:], pD[:, 256:512], ALU.add, [bye, bpD[2]], [bye])
                ACT(sza[:], pz[:, 0:256], AF.Silu, [bpz], [bsza])
                TT("dve", ye[:], ye[:], sza[:], ALU.mult, [bye, bsza], [bye])
                if DBG < 6:
                    continue
                ssq = sm[:, 40:41]
                rstd = sm[:, 41:42]
                S.act(lambda e, ssq=ssq: e.activation(out=junk[:], in_=ye[:], func=AF.Square, accum_out=ssq), [bye], [bjunk, bsm])
                ACT(rstd, ssq, AF.Sqrt, [bsm], [bsm], scale=1.0 / 256, bias=float(SSD_EPS))
                S.dve(lambda e, rstd=rstd: e.reciprocal(out=rstd, in_=rstd), [bsm], [bsm])
                TSC("dve", ynb[:], ye[:], rstd, None, ALU.mult, None, [bye, bsm], [bynb])
                if DBG < 7:
                    continue
                for i in range(2):
                    ts_, bts = tslot()
                    TR(ts_, ynb[:, i * 128:(i + 1) * 128], ident, [bynb, bcst], [bts])
                    TSC("dve", ya_out[:, i, sl], ts_, pcol(PC_NORMW + i), None, ALU.mult, None, [bts, bpv], [bya_out])
            S.dma(lambda e, t0=t0: e.dma_start(out=yaT.rearrange("(i p) t -> p i t", p=128)[:, :, t0:t0 + TB], in_=ya_out[:]),
                  R=[bya_out], q="act", is_output=True)

        if "rwkv" in parts:
            for i in range(5):
                mucol = (PC_MU + i) if i < 4 else PC_MULORA
                TSC("dve", tmpR[:], Rw[i][:, 0:TB], pcol(mucol), None, ALU.mult, None, [bRw[i], bpv], [btmpR])
                STT(rwm[i][0][:], Rw[i][:, 1:TB + 1], omv[:, mucol:mucol + 1], tmpR[:], ALU.mult, ALU.add, [bRw[i], bpv, btmpR], [rwm[i][1]])
            (r_m, br_m), (k_m, bk_m), (v_m, bv_m), (z_m, bz_m), (xwa, bxwa) = rwm
            ACT(lw[0:64, :], xwa[0:64, :], AF.Tanh, [bxwa], [blw])
            CP("dve", lw[64:128, :], xwa[64:128, :], [bxwa], [blw])
            pw_, bpw_ = pg.next()
            MM(pw_[:, 0:TB], w2a2[0:64, :], lw[0:64, :], True, True, [bw2, blw], [bpw_])
            ACT(ld[:], pw_[:, 0:TB], AF.Sigmoid, [bpw_, bpv], [bld], bias=pcol(PC_W0))
            TSC("dve", ld[:], ld[:], -EXPM05, None, ALU.mult, None, [bld], [bld])
            pa_, bpa_ = pg.next()
            MM(pa_[:, 0:TB], w2a2[64:128, :], lw[64:128, :], True, True, [bw2, blw], [bpa_])
            ACT(asig[:], pa_[:, 0:TB], AF.Sigmoid, [bpa_, bpv], [basig], bias=pcol(PC_A0))
            TSC("dve", kk[:], k_m[:], pcol(PC_KK), None, ALU.mult, None, [bk_m, bpv], [bkk])
            TT("dve", kk2[:], kk[:], kk[:], ALU.mult, [bkk], [bkk2])
            ps_, bps_ = pg.next()
            MM(ps_[:, 0:TB], bones_bf, kk2[:], True, True, [bcst, bkk2], [bps_])
            TSC("dve", rs[:], ps_[:, 0:TB], 1e-24, None, ALU.max, None, [bps_], [brs])
            ACT(rs[:], rs[:], AF.Sqrt, [brs], [brs])
            S.dve(lambda e: e.reciprocal(out=rs[:], in_=rs[:]), [brs], [brs])
            TT("dve", kkn[:], kk[:], rs[:], ALU.mult, [bkk, brs], [bkkn])
            TSC("dve", kp[:], asig[:], pcol(PC_KA), omv[:, PC_KA:PC_KA + 1], ALU.mult, ALU.add, [basig, bpv], [bkp])
            TT("dve", kp[:], kp[:], k_m[:], ALU.mult, [bkp, bk_m], [bkp])
            TT("dve", bvec[:], kkn[:], asig[:], ALU.mult, [bkkn, basig], [bbvec])
            TT("dve", tmpR[:], r_m[:], kp[:], ALU.mult, [br_m, bkp], [btmpR])
            TSC("dve", prod[:], tmpR[:], pcol(PC_RK), None, ALU.mult, None, [btmpR, bpv], [bprod])
            pr_, bpr_ = pg.next()
            MM(pr_[:, 0:TB], bones_bf, prod[:], True, True, [bcst, bprod], [bpr_])
            TT("dve", rv[:], pr_[:, 0:TB], v_m[:], ALU.mult, [bpr_, bv_m], [brv])
            ACT(sz[:], z_m[:], AF.Silu, [bz_m], [bsz])
            for c in range(nch):
                sl = slice(c * 128, (c + 1) * 128)
                S.dve(lambda e, sl=sl: e.tensor_tensor_scan(out=LW[:, sl], data0=ones[:, 0:128], data1=ld[:, sl], initial=0.0,
                                                           op0=ALU.mult, op1=ALU.add), [bones, bld], [bLW])
            ACT(Ep[:], LW[:], AF.Exp, [bLW], [bEp])
            ACT(Em[:], LW[:], AF.Exp, [bLW], [bEm], scale=-1.0)
            TT("dve", tmpR[:], LW[:], ld[:], ALU.subtract, [bLW, bld], [btmpR])
            ACT(Epr[:], tmpR[:], AF.Exp, [btmpR], [bEpr])
            for c in range(nch):
                sl = slice(c * 128, (c + 1) * 128)
                ACT(Ee[:, sl], LW[:, sl], AF.Exp, [bLW], [bEe], scale=-1.0, bias=LW[:, c * 128 + 127:c * 128 + 128])
            v3 = lambda t: t[:].rearrange("p (c t) -> p c t", t=128)
            STT(AR[:, :, 0, :], v3(kkn), -1.0, v3(Epr), ALU.mult, ALU.mult, [bkkn, bEpr], [bAR])
            TT("dve", AR[:, :, 1, :], v3(r_m), v3(Ep), ALU.mult, [br_m, bEp], [bAR])
            TT("dve", BK[:, :, 0, :], v3(bvec), v3(Em), ALU.mult, [bbvec, bEm], [bBK])
            TT("dve", BK[:, :, 1, :], v3(kp), v3(Em), ALU.mult, [bkp, bEm], [bBK])
            TT("dve", bh[:], bvec[:], Ee[:], ALU.mult, [bbvec, bEe], [bbh])
            TT("dve", kh[:], kp[:], Ee[:], ALU.mult, [bkp, bEe], [bkh])
            CP("act", vb[:], v_m[:], [bv_m], [bvb])
            for c in (range(nch) if RDBG >= 2 else ()):
                sl = slice(c * 128, (c + 1) * 128)
                for (src, bsrc, dst, bdst) in ((vb, bvb, vT, bvT), (bh, bbh, bhT, bbhT), (kh, bkh, khT, bkhT)):
                    ts_, bts = tslot()
                    TR(ts_, src[:, sl], ident, [bsrc, bcst], [bts])
                    CP("act", dst[:], ts_, [bts], [bdst])
                for h in (range(2) if RDBG >= 3 else ()):
                    hs = slice(h * 64, (h + 1) * 64)
                    art = AR[hs, c, :, :].rearrange("p a t -> p (a t)")
                    MM(pA[:, 0:256], BK[hs, c, 0, :], art, True, True, [bBK, bAR], [bpA])
                    MM(pA[:, 256:512], BK[hs, c, 1, :], art, True, True, [bBK, bAR], [bpA])
                    MM(pI[:, 0:128], AR[hs, c, 0, :], BK[hs, c, 0, :], True, True, [bBK, bAR], [bpI[0]])
                    m4m, bm4m = M4m[h]
                    TT("dve", m4m[:], pA[:], m4[:], ALU.mult, [bpA, bcst], [bm4m])
                    P_, Q_, R_ = Pq[h], Qq[h], Rq[h]
                    TT("dve", Q_[0][0][:], pI[:, 0:128], mgt, ALU.mult, [bpI[0], bcst], [Q_[0][1]])
                    TT("dve", R_[0][0][:], m4m[:, 0:128], ident, ALU.add, [bm4m, bcst], [R_[0][1]])
                    Pc, bPc = m4m[:, 0:128], bm4m
                    Qc, bQc = Q_[0][0][:], Q_[0][1]
                    Rc, bRc = R_[0]
                    for lev in (range(1, 7) if RDBG >= 4 else ()):
                        Pn, bPn = P_[lev % 2]
                        Qn, bQn = Q_[lev % 2]
                        Rn, bRn = R_[lev % 2]
                        if lev < 6:
                            MM(pT[:, 0:128], Qc, Pc, True, True, [bQc, bPc], [bpT[0]])
                        MM(pX[:, 0:128], Pc, Qc, True, True, [bQc, bPc], [bpX])
                        if lev < 6:
                            CP("act", Pn[:], pT[:, 0:128], [bpT[0]], [bPn])
                        CP("dve", Qn[:], pX[:, 0:128], [bpX], [bQn])
                        MM(pD[:, 0:128], Qn[:], Rc[:], True, True, [bQn, bRc], [bpD[0]])
                        TT("dve", Rn[:], Rc[:], pD[:, 0:128], ALU.add, [bRc, bpD[0]], [bRn])
                        Pc, bPc, Qc, bQc, Rc, bRc = Pn[:], bPn, Qn[:], bQn, Rn, bRn
                    if RDBG < 5:
                        continue
                    hcol = slice(h * 64, (h + 1) * 64)
                    zs, bzs = Zs[h]
                    us, bus = Us[h]
                    pZ, pU, pY = pS[:, 0:64], pA[:, 0:64], pI[:, 0:64]
                    pStt, bpSt = pg.next()
                    pSt = pStt[:, 0:64]
                    MM(pZ, AR[hs, c, 0, :], STb[hs, :], True, False, [bAR, bST[h]], [bpS[0]])
                    MM(pZ, m4m[:, 256:384], vT[:, hcol], False, True, [bm4m, bvT], [bpS[0]])
                    CP("act", zs[:], pZ, [bpS[0]], [bzs])
                    MM(pU, Rc[:], zs[:], True, True, [bRc, bzs], [bpA])
                    CP("act", us[:], pU, [bpA], [bus])
                    MM(pY, AR[hs, c, 1, :], STb[hs, :], True, False, [bAR, bST[h]], [bpI[0]])
                    MM(pY, m4m[:, 128:256], us[:], False, False, [bm4m, bus], [bpI[0]])
                    MM(pY, m4m[:, 384:512], vT[:, hcol], False, True, [bm4m, bvT], [bpI[0]])
                    MM(pSt[hs, :], bhT[:, hcol], us[:], True, False, [bbhT, bus], [bpSt])
                    MM(pSt[hs, :], khT[:, hcol], vT[:, hcol], False, True, [bkhT, bvT], [bpSt])
                    wc_ = Ep[hs, c * 128 + 127:c * 128 + 128]
                    STT(ST[hs, :], ST[hs, :], wc_, pSt[hs, :], ALU.mult, ALU.add, [bEp, bpSt, bST[h]], [bST[h]])
                    CP("act", STb[hs, :], ST[hs, :], [bST[h]], [bST[h]])
                    if RDBG < 6:
                        continue
                    S.dve(lambda e, pY=pY, h=h: e.bn_stats(out=st6[:, h * 8:h * 8 + 6], in_=pY), [bpI[0]], [bst6])
                    S.dve(lambda e, h=h: e.bn_aggr(out=st6[:, h * 8 + 6:h * 8 + 8], in_=st6[:, h * 8:h * 8 + 6]), [bst6], [bst6])
                    var = st6[:, h * 8 + 7:h * 8 + 8]
                    ACT(var, var, AF.Sqrt, [bst6], [bst6], bias=float(RW_EPS))
                    S.dve(lambda e, var=var: e.reciprocal(out=var, in_=var), [bst6], [bst6])
                    TSC("dve", ynr[:, hcol], pY, st6[:, h * 8 + 6:h * 8 + 7], var, ALU.subtract, ALU.mult, [bpI[0], bst6], [bynr])
                if RDBG < 7:
                    continue
                ts_, bts = tslot()
                TR(ts_, ynr[:], ident, [bynr, bcst], [bts])
                TSC("dve", yo[:], ts_, pcol(PC_LNW), pcol(PC_LNB), ALU.mult, ALU.add, [bts, bpv], [byo])
                TT("dve", yo[:], yo[:], rv[:, sl], ALU.add, [byo, brv], [byo])
                TT("dve", yc_out[:, sl], yo[:], sz[:, sl], ALU.mult, [byo, bsz], [byc_out])
            S.dma(lambda e, t0=t0: e.dma_start(out=ycT[:, t0:t0 + TB], in_=yc_out[:]), R=[byc_out], q="act", is_output=True)

    pst, bps = pg.next()
    MM(pst[:, 0:128], ident, ident, True, True, [bcst], [bps])
    S.emit()
    S.close()
    return nc


OFF_ZA, OFF_XS, OFF_B, OFF_C, OFF_DT, OFF_UB, OFF_ZB = 0, 2048, 4096, 5120, 6144, 6176, 7200
OFF_R, OFF_K, OFF_V, OFF_Z, OFF_PW, OFF_PA, OFF_G = 8224, 9248, 10272, 11296, 12320, 12384, 12448


def consts_p1():
    p = np.arange(128)[:, None]
    f = np.arange(128)[None, :]
    cst = np.zeros((128, CST_N), np.float32)
    cst[:, CS_ID:CS_ID + 128] = (p == f)
    cst[:, CS_LE:CS_LE + 128] = (p <= f)
    cst[:, CS_LT:CS_LT + 128] = (p < f)
    cst[:, CS_GT:CS_GT + 128] = (p > f)
    cst[:, CS_BO:CS_BO + 128] = (p // 64 == f // 64)
    return cst


def prep_p1(P, c):
    w_in = P["w_in"]
    cols = []
    for o in (OFF_XS + 256 * c, OFF_XS + 256 * c + 128, OFF_B + 128 * c, OFF_C + 128 * c, OFF_UB + 128 * c,
              OFF_R + 128 * c, OFF_K + 128 * c, OFF_V + 128 * c, OFF_Z + 128 * c, OFF_PW):
        cols.append(np.arange(o, o + 128))
    wsub = w_in[:, np.concatenate(cols)]
    w1cm = tile_w(wsub)
    tcols = np.concatenate([np.arange(OFF_ZA + 256 * c, OFF_ZA + 256 * c + 256), np.arange(OFF_DT + 4 * c, OFF_DT + 4 * c + 4)])
    w1tm = np.ascontiguousarray(w_in[:, tcols].reshape(16, 128, 260).transpose(1, 0, 2).reshape(128, 16 * 260))
    pv = np.zeros((128, PV1_N), np.float32)
    chans = [np.arange(256 * c, 256 * c + 128), np.arange(256 * c + 128, 256 * c + 256),
             np.arange(2048 + 128 * c, 2048 + 128 * c + 128), np.arange(3072 + 128 * c, 3072 + 128 * c + 128)]
    for i, ch in enumerate(chans):
        for k in range(4):
            pv[:, PC_CONVW + 4 * i + k] = P["ssd_conv_w"][k, ch]
        pv[:, PC_CONVB + i] = P["ssd_conv_b"][ch]
    for i in range(2):
        pv[:, PC_NORMW + i] = P["ssd_norm_w"][256 * c + 128 * i:256 * c + 128 * (i + 1)]
    sl = slice(128 * c, 128 * c + 128)
    pv[:, PC_DS5] = P["s5_d"][sl]
    for gp in range(4):
        for g2 in range(2):
            g = 8 * c + 2 * gp + g2
            rows = slice(64 * g2, 64 * g2 + 64)
            pv[rows, PC_S5P + 3 * gp + 0] = P["s5_lambda_re"][g]
            pv[rows, PC_S5P + 3 * gp + 1] = P["s5_lambda_im"][g]
            pv[rows, PC_S5P + 3 * gp + 2] = P["s5_log_dt"][g]
    for i in range(4):
        pv[:, PC_MU + i] = P["rwkv_mu"][i, sl]
    pv[0:64, PC_MULORA] = P["rwkv_mu_lora"][0]
    pv[64:128, PC_MULORA] = P["rwkv_mu_lora"][1]
    pv[:, PC_W0] = P["rwkv_w0"][sl]
    pv[:, PC_A0] = P["rwkv_a0"][sl]
    pv[:, PC_KK] = P["rwkv_k_k"][sl]
    pv[:, PC_KA] = P["rwkv_k_a"][sl]
    pv[:, PC_RK] = P["rwkv_r_k"].reshape(-1)[sl]
    pv[:, PC_LNW] = P["rwkv_lnx_w"][sl]
    pv[:, PC_LNB] = P["rwkv_lnx_b"][sl]
    bt = np.zeros((128, BT1_N), np.float32)
    bt[:, BT_DTB:BT_DTB + 4] = P["ssd_dt_bias"][4 * c:4 * c + 4][None, :]
    bt[:, BT_ALOG:BT_ALOG + 4] = P["ssd_a_log"][4 * c:4 * c + 4][None, :]
    bt[:, BT_D:BT_D + 256] = np.repeat(P["ssd_d"][4 * c:4 * c + 4], 64)[None, :]
    w2a2 = np.concatenate([P["rwkv_w2"][:, sl], P["rwkv_a2"][:, sl]], axis=0).astype(np.float32)
    s5b = np.zeros((128, 8 * 128), np.float32)
    s5c = np.zeros((128, 8 * 128), np.float32)
    for gp in range(4):
        for g2 in range(2):
            gl = 2 * gp + g2
            g = 8 * c + gl
            s5b[gl * 16:(gl + 1) * 16, gp * 128 + g2 * 64:gp * 128 + (g2 + 1) * 64] = P["s5_b_re"][g].T
            s5b[gl * 16:(gl + 1) * 16, (4 + gp) * 128 + g2 * 64:(4 + gp) * 128 + (g2 + 1) * 64] = P["s5_b_im"][g].T
            s5c[g2 * 64:(g2 + 1) * 64, gp * 128 + gl * 16:gp * 128 + (gl + 1) * 16] = P["s5_c_re"][g].T
            s5c[g2 * 64:(g2 + 1) * 64, (4 + gp) * 128 + gl * 16:(4 + gp) * 128 + (gl + 1) * 16] = P["s5_c_im"][g].T
    return dict(w1cm=w1cm, w1tm=w1tm, pv=pv, bt=bt, w2a2=np.ascontiguousarray(w2a2), s5b=s5b, s5c=s5c)


_PROGS = {}


def _prog(key, fn):
    if key not in _PROGS:
        _PROGS[key] = fn()
    return _PROGS[key]


def kernel(**inputs):
    x = np.asarray(inputs["x"], np.float32)
    Bsz, S_len, d = x.shape
    NT = Bsz * S_len
    NTc = NT // NCORES
    TB = 256
    cores = list(range(NCORES))
    cst = consts_p1()
    xT = np.ascontiguousarray(x.reshape(NT, d).T)
    for l in range(DEPTH):
        P = {k: np.asarray(v[l], np.float32) for k, v in inputs.items() if k != "x"}
        nc1 = _prog(("p1", NT, S_len, TB), lambda: build_p1(NT, S_len, TB, False))
        maps = []
        for c in cores:
            m = prep_p1(P, c)
            m["xT"] = xT
            m["cst"] = cst
            maps.append(m)
        r1 = run_bass_kernel_spmd(nc1, maps, core_ids=cores).results
        yaT = np.concatenate([r["yaT"] for r in r1], axis=0).astype(np.float32)
        gbT = np.concatenate([r["gbT"] for r in r1], axis=0).astype(np.float32)
        ycT = np.concatenate([r["ycT"] for r in r1], axis=0).astype(np.float32)
        del r1, maps
        nc2 = _prog(("p2", NTc), lambda: build_p2(NTc))
        w_in = P["w_in"]
        common = dict(wg=tile_w(w_in[:, OFF_G:]), wzb=tile_w(w_in[:, OFF_ZB:OFF_ZB + 1024]), wglu=tile_w(P["s5_w_glu"]),
                      wba=tile_w(P["w_branch_a"]), wbb=tile_w(P["w_branch_b"]), wbc=tile_w(P["w_branch_c"]), wo=tile_w(P["w_out"]))
        pv = np.zeros((128, PV2_N), np.float32)
        pv[:, PV2_GATEB:PV2_GATEB + 48] = pcol(P["gate_b"].reshape(-1))
        pv[:, PV2_BGLU:PV2_BGLU + 8] = pcol(P["s5_b_glu"])
        pv[:, PV2_LNG:PV2_LNG + 16] = pcol(P["ln_g"])
        pv[:, PV2_LNB:PV2_LNB + 16] = pcol(P["ln_b"])
        common["pv"] = pv
        maps = []
        for c in cores:
            sl = slice(c * NTc, (c + 1) * NTc)
            m = dict(common)
            m.update(xT=np.ascontiguousarray(xT[:, sl]), yaT=np.ascontiguousarray(yaT[:, sl]),
                     gbT=np.ascontiguousarray(gbT[:, sl]), ycT=np.ascontiguousarray(ycT[:, sl]))
            maps.append(m)
        r2 = run_bass_kernel_spmd(nc2, maps, core_ids=cores).results
        xT = np.concatenate([r["xnT"] for r in r2], axis=1)
        del r2, maps, common
    return np.ascontiguousarray(xT.T).reshape(Bsz, S_len, d).astype(np.float32)
```
